# Optimizing a Trainium2 kernel written in Bass

```python
import jax, jax.numpy as jnp
from jax import lax
import numpy as np

D_MODEL = 1024
BATCH = 8
SEQ = 4096
DEPTH = 1

CHUNK = 64
N_LEFT_CHUNKS = 8
BAND = (N_LEFT_CHUNKS + 1) * CHUNK
D_MIX = D_MODEL
D_ATTN = D_MIX // 2
HEAD_DIM = 64
N_HEADS = D_ATTN // HEAD_DIM
D_POOL = D_MIX - D_ATTN
POOL_WINDOWS = (2, 4, 8, 16)
N_POOL_GROUPS = len(POOL_WINDOWS)
POOL_GROUP_DIM = D_POOL // N_POOL_GROUPS
REL_CLIP = 128
N_REL = 2 * REL_CLIP + 1
D_FF = 2816
D_IN = 3 * D_ATTN + D_POOL
EPS = 1e-6
NEG_INF = -1e30

kernel_name = "hybrid_chunk_attn_pool_macaron"


def rmsnorm(x, g):
    xf = x.astype(jnp.float32)
    y = xf * lax.rsqrt(jnp.mean(xf * xf, axis=-1, keepdims=True) + EPS)
    return (y * g.astype(jnp.float32)).astype(x.dtype)


def swiglu_ffn(x, w_gate, w_up, w_down):
    return (jax.nn.silu(x @ w_gate) * (x @ w_up)) @ w_down


def multiscale_pool(u, pool_w, pool_scale):
    B, S, _ = u.shape
    uf = u.astype(jnp.float32)
    cs = jnp.cumsum(uf, axis=1)
    cs = jnp.concatenate([jnp.zeros_like(cs[:, :1]), cs], axis=1)
    t = jnp.arange(S)
    outs = []
    for g, w in enumerate(POOL_WINDOWS):
        lo, hi = g * POOL_GROUP_DIM, (g + 1) * POOL_GROUP_DIM
        c = cs[:, :, lo:hi]
        start = jnp.maximum(t + 1 - w, 0)
        win_sum = c[:, 1:] - c[:, start]
        count = (t + 1 - start).astype(jnp.float32)
        outs.append(win_sum / count[None, :, None] - uf[:, :, lo:hi])
    d = jnp.stack(outs, axis=2)
    y = jnp.einsum('bsgc,gcd->bsgd', d, pool_w.astype(jnp.float32)).reshape(B, S, D_POOL)
    return (y * pool_scale.astype(jnp.float32)).astype(u.dtype)


def chunked_attention(q, k, v, q_gain, k_gain, rel_bias):
    B, S, _ = q.shape
    nc = S // CHUNK
    q = rmsnorm(q.reshape(B, S, N_HEADS, HEAD_DIM), q_gain)
    k = rmsnorm(k.reshape(B, S, N_HEADS, HEAD_DIM), k_gain)
    v = v.reshape(B, S, N_HEADS, HEAD_DIM)
    pad = N_LEFT_CHUNKS * CHUNK
    kp = jnp.pad(k, ((0, 0), (pad, 0), (0, 0), (0, 0)))
    vp = jnp.pad(v, ((0, 0), (pad, 0), (0, 0), (0, 0)))
    qc = q.reshape(B, nc, CHUNK, N_HEADS, HEAD_DIM)
    qpos = jnp.arange(CHUNK)[:, None] + pad
    kpos = jnp.arange(BAND)[None, :]
    rel_idx = jnp.clip(qpos - kpos, -REL_CLIP, REL_CLIP) + REL_CLIP
    bias = rel_bias.astype(jnp.float32)[:, rel_idx]
    scale = HEAD_DIM ** -0.5

    def one_chunk(c):
        qb = lax.dynamic_index_in_dim(qc, c, axis=1, keepdims=False)
        kb = lax.dynamic_slice_in_dim(kp, c * CHUNK, BAND, axis=1)
        vb = lax.dynamic_slice_in_dim(vp, c * CHUNK, BAND, axis=1)
        s = jnp.einsum('bqhd,bkhd->bhqk', qb, kb).astype(jnp.float32) * scale + bias
        valid = (c * CHUNK + kpos - pad) >= 0
        s = jnp.where(valid[None, None], s, NEG_INF)
        p = jax.nn.softmax(s, axis=-1)
        return jnp.einsum('bhqk,bkhd->bqhd', p.astype(vb.dtype), vb)

    o = lax.map(one_chunk, jnp.arange(nc))
    return o.transpose(1, 0, 2, 3, 4).reshape(B, S, D_ATTN)


def hybrid_mixer(h, w_in, q_gain, k_gain, rel_bias, pool_w, pool_scale, w_out):
    proj = h @ w_in
    q = proj[..., :D_ATTN]
    k = proj[..., D_ATTN:2 * D_ATTN]
    v = proj[..., 2 * D_ATTN:3 * D_ATTN]
    u = proj[..., 3 * D_ATTN:]
    a = chunked_attention(q, k, v, q_gain, k_gain, rel_bias)
    p = multiscale_pool(u, pool_w, pool_scale)
    return jnp.concatenate([a, p], axis=-1) @ w_out


def setup_inputs(seed: int = 0) -> dict:
    key = jax.random.key(seed)
    ks = jax.random.split(key, 20)
    f32 = jnp.float32
    L = DEPTH

    def nrm(k, shape, fan_in):
        return jax.random.normal(k, shape, f32) * fan_in ** -0.5

    def gain(k, shape):
        return 1.0 + 0.1 * jax.random.normal(k, shape, f32)

    return {
        "x": jax.random.normal(ks[0], (BATCH, SEQ, D_MODEL), f32),
        "ffn1_norm": gain(ks[1], (L, D_MODEL)),
        "ffn1_w_gate": nrm(ks[2], (L, D_MODEL, D_FF), D_MODEL),
        "ffn1_w_up": nrm(ks[3], (L, D_MODEL, D_FF), D_MODEL),
        "ffn1_w_down": nrm(ks[4], (L, D_FF, D_MODEL), D_FF),
        "mix_norm": gain(ks[5], (L, D_MODEL)),
        "w_in": nrm(ks[6], (L, D_MODEL, D_IN), D_MODEL),
        "q_norm": gain(ks[7], (L, HEAD_DIM)),
        "k_norm": gain(ks[8], (L, HEAD_DIM)),
        "rel_bias": 0.5 * jax.random.normal(ks[9], (L, N_HEADS, N_REL), f32),
        "pool_w": nrm(ks[10], (L, N_POOL_GROUPS, POOL_GROUP_DIM, POOL_GROUP_DIM), POOL_GROUP_DIM),
        "pool_scale": gain(ks[11], (L, D_POOL)),
        "w_out": nrm(ks[12], (L, D_MIX, D_MODEL), D_MIX),
        "ffn2_norm": gain(ks[13], (L, D_MODEL)),
        "ffn2_w_gate": nrm(ks[14], (L, D_MODEL, D_FF), D_MODEL),
        "ffn2_w_up": nrm(ks[15], (L, D_MODEL, D_FF), D_MODEL),
        "ffn2_w_down": nrm(ks[16], (L, D_FF, D_MODEL), D_FF),
        "final_norm": gain(ks[17], (L, D_MODEL)),
    }


def reference(x, ffn1_norm, ffn1_w_gate, ffn1_w_up, ffn1_w_down, mix_norm, w_in,
              q_norm, k_norm, rel_bias, pool_w, pool_scale, w_out,
              ffn2_norm, ffn2_w_gate, ffn2_w_up, ffn2_w_down, final_norm):
    for l in range(DEPTH):
        x = x + 0.5 * swiglu_ffn(rmsnorm(x, ffn1_norm[l]), ffn1_w_gate[l], ffn1_w_up[l], ffn1_w_down[l])
        x = x + hybrid_mixer(rmsnorm(x, mix_norm[l]), w_in[l], q_norm[l], k_norm[l], rel_bias[l],
                             pool_w[l], pool_scale[l], w_out[l])
        x = x + 0.5 * swiglu_ffn(rmsnorm(x, ffn2_norm[l]), ffn2_w_gate[l], ffn2_w_up[l], ffn2_w_down[l])
        x = rmsnorm(x, final_norm[l])
    return x
```

```python
import numpy as np
from contextlib import ExitStack
import concourse.bass as bass
import concourse.mybir as mybir
from concourse.bass_utils import run_bass_kernel_spmd

F32 = mybir.dt.float32
BF16 = mybir.dt.bfloat16
ALU = mybir.AluOpType
AF = mybir.ActivationFunctionType

D = 1024
DFF = 2816
NFC = DFF // 128
NDC = D // 128
T = 512
NS = T // 128
NW = 8
EPS = 1e-6
POOL_W = (2, 4, 8, 16)
G1, GM, G2, QG, KG, PSC = 0, 8, 16, 24, 25, 26


class Ev:
    def __init__(self, nc, es, name):
        self.sem = es.enter_context(nc.semaphore(name))
        self.n = 0
        self.name = name


class Q:
    def __init__(self, nc, es, name, serial=False):
        self.name = name
        self.ops = []
        self.waited = {}
        self.ev = Ev(nc, es, "ev_" + name)
        self.serial = serial

    def wait(self, dep):
        if dep is None:
            return
        ev, val = dep
        if val <= 0:
            return
        if self.waited.get(ev.name, 0) >= val:
            return
        self.waited[ev.name] = val
        self.ops.append(("w", ev, val))

    def op(self, fn, inc=True, nowait=False):
        if self.serial and not nowait:
            self.wait((self.ev, self.ev.n))
        if inc:
            self.ev.n += 1
            self.ops.append(("o", fn, self.ev, 1))
            return (self.ev, self.ev.n)
        self.ops.append(("o", fn, None, 0))
        return None

    def dma(self, fn, ev):
        ev.n += 16
        self.ops.append(("o", fn, ev, 16))
        return (ev, ev.n)

    def replay(self, eng):
        for o in self.ops:
            if o[0] == "w":
                eng.wait_ge(o[1].sem, o[2])
            else:
                ins = o[1](eng)
                if o[2] is not None:
                    ins.then_inc(o[2].sem, o[3])


class Bank:
    def __init__(self, ap):
        self.ap = ap
        self.rel = []

    def begin(self, PE):
        for d in self.rel:
            PE.wait(d)
        self.rel = []


def build_program(S):
    NT = S // T
    nc = bass.Bass("TRN2", target_bir_lowering=False)
    dt = nc.dram_tensor
    x_h = dt("x", [S, D], F32, kind="ExternalInput").ap()
    wg1_h = dt("wg1", [D, DFF], F32, kind="ExternalInput").ap()
    wu1_h = dt("wu1", [D, DFF], F32, kind="ExternalInput").ap()
    wd1_h = dt("wd1", [DFF, D], F32, kind="ExternalInput").ap()
    wg2_h = dt("wg2", [D, DFF], F32, kind="ExternalInput").ap()
    wu2_h = dt("wu2", [D, DFF], F32, kind="ExternalInput").ap()
    wd2_h = dt("wd2", [DFF, D], F32, kind="ExternalInput").ap()
    win_h = dt("win", [D, 2048], F32, kind="ExternalInput").ap()
    wout_h = dt("wout", [D, D], F32, kind="ExternalInput").ap()
    poolw_h = dt("poolw", [128, 512], F32, kind="ExternalInput").ap()
    prm_h = dt("prm", [128, 32], F32, kind="ExternalInput").ap()
    gfin_h = dt("gfin", [128, D], F32, kind="ExternalInput").ap()
    cb_h = dt("cb", [128, 256], F32, kind="ExternalInput").ap()
    invc_h = dt("invc", [128, 64], F32, kind="ExternalInput").ap()
    bias_h = dt("biasT", [128, 8 * 5 * 128], F32, kind="ExternalInput").ap()
    y_h = dt("y", [S, D], F32, kind="ExternalOutput").ap()

    with ExitStack() as es:
        E = es.enter_context
        sb = lambda name, shape, dtype: E(nc.sbuf_tensor(name, shape, dtype))
        xres2 = sb("xres", [128, 2, NS, D], F32)
        junk = sb("junk", [128, D], BF16)
        st = sb("st", [128, 48], F32)
        xn = sb("xn", [128, 2, D], BF16)
        hT = sb("hT", [128, NDC, T], BF16)
        hid = sb("hid", [128, NFC, T], BF16)
        sg = sb("sg", [128, 2, T], F32)
        wr = sb("wr", [128, NW, 2048], BF16)
        prm = sb("prm_s", [128, 32], F32)
        epsc = sb("epsc", [128, 2], F32)
        gfin = sb("gfin_s", [128, D], F32)
        cb = sb("cb_s", [128, 256], BF16)
        poolw = sb("poolw_s", [128, 4, 128], BF16)
        invc = sb("invc_s", [128, 4, 16], F32)
        etab = sb("etab", [128, 8, 5, 128], F32)
        qn = sb("qn", [128, 4, T], BF16)
        kT = sb("kT", [128, 4, 2 * T], BF16)
        V = sb("V", [128, 8, 8, 65], BF16)
        uT = sb("uT", [128, 4, 16 + T], F32)
        ptmp = sb("ptmp", [128, 2, 16 + T], F32)
        ptmp2 = sb("ptmp2", [128, 16], F32)
        dT = sb("dT", [128, 4, T], BF16)
        apT = sb("apT", [128, 8, T], BF16)
        atok = sb("atok", [128, 2, 512], BF16)
        expS = sb("expS", [128, 2, 640], F32)
        PT = sb("PT", [128, 3, 640], BF16)
        sq = sb("sq", [128, 2, T], BF16)
        rr = sb("rr", [128, 2, T], F32)
        rec = sb("rec", [128, 16], F32)
        ostage = sb("ostage", [128, 4, D], F32)
        ps = E(nc.psum_tensor("ps", [128, 4096], F32))

        PE = Q(nc, es, "pe")
        ACT = Q(nc, es, "act", serial=True)
        DVE = Q(nc, es, "dve", serial=True)
        POOL = Q(nc, es, "pool")
        SP = Q(nc, es, "sp")
        xld = [Ev(nc, es, "xld0"), Ev(nc, es, "xld1")]
        cld = Ev(nc, es, "cld")
        cld2 = Ev(nc, es, "cld2")
        cld3 = Ev(nc, es, "cld3")
        ost = [Ev(nc, es, f"ost{k}") for k in range(4)]
        wld = [Ev(nc, es, f"wld{s}") for s in range(NW)]
        block = E(nc.Block())

        banks = [Bank(ps[:, b * 512:(b + 1) * 512]) for b in range(8)]
        ident = cb[:, 0:128]
        bdiag = cb[:, 128:256]

        xl0 = SP.dma(lambda e: e.dma_start(
            out=xres2[:, 0], in_=x_h[0:T, :].rearrange("(s p) d -> p s d", p=128)), xld[0])
        c_prm = SP.dma(lambda e: e.dma_start(out=prm[:], in_=prm_h[:, :]), cld)
        d_c = []
        d_c.append(SP.dma(lambda e: e.dma_start(out=gfin[:], in_=gfin_h[:, :]), cld3))
        d_c.append(SP.dma(lambda e: e.dma_start(out=invc[:].rearrange("p g t -> p (g t)"), in_=invc_h[:, :]), cld3))
        d_c.append(SP.dma(lambda e: e.dma_start(out=etab[:].rearrange("p h g q -> p (h g q)"), in_=bias_h[:, :]), cld3))
        c_all = d_c[-1]
        d2 = POOL.dma(lambda e: e.dma_start(out=cb[:], in_=cb_h[:, :]), cld2)
        d2 = POOL.dma(lambda e: e.dma_start(out=poolw[:].rearrange("p g d -> p (g d)"), in_=poolw_h[:, :]), cld2)
        c2_all = d2

        m0 = DVE.op(lambda e: e.memset(epsc[:], EPS))
        m1 = DVE.op(lambda e: e.memset(V[:, :, :, 64:65], 1.0))
        m2 = DVE.op(lambda e: e.memset(uT[:, :, 0:16], 0.0))
        m3 = DVE.op(lambda e: e.memset(st[:], 1.0))
        setup_dve = m3
        etab_state = {"ready": None}

        def etab_exp(h):
            ACT.wait(c_all)
            etab_state["ready"] = ACT.op(lambda e, h=h: e.activation(
                out=etab[:, h].rearrange("p g q -> p (g q)"), in_=etab[:, h].rearrange("p g q -> p (g q)"),
                func=AF.Exp))
        ACT.wait(setup_dve)
        ACT.op(lambda e: e.activation(out=st[:, 40:41], in_=st[:, 41:42], func=AF.Square))
        PE.wait(c2_all)
        DVE.wait(c_prm)

        wstate = {"n": 0, "free": [None] * NW, "loads": [0] * NW}

        def wacquire(src_ap, view):
            u = wstate["n"]
            wstate["n"] += 1
            s = u % NW
            POOL.wait(wstate["free"][s])
            n_el = 1
            for d_ in src_ap.shape[1:]:
                n_el *= d_
            dst = view(wr[:, s, 0:n_el])
            dep = POOL.dma(lambda e, dst=dst, src=src_ap: e.dma_start(out=dst, in_=src), wld[s])
            return s, dst, dep

        def wrelease(s, dep):
            wstate["free"][s] = dep

        def gate_src(w_h, p):
            return w_h.rearrange("(c p) f -> p c f", p=128)[:, :, 256 * p:256 * p + 256]

        def rows_src(w_h, c0, ncnk, half):
            return w_h.rearrange("(c p) d -> p c d", p=128)[:, c0:c0 + ncnk, half * 512:half * 512 + 512]

        v_c256 = lambda ap: ap.rearrange("p (c f) -> p c f", f=256)
        v_c512 = lambda ap: ap.rearrange("p (c f) -> p c f", f=512)

        state = {"xn_free": [None, None], "tpi": 0}

        def make_norm(n, gcol, xdep, xres):
            ready = [None] * NS
            stA = {}

            def stage_a(ts):
                k = n * 4 + ts
                ACT.wait(xdep[ts])
                a1 = ACT.op(lambda e, ts=ts, k=k: e.activation(out=junk[:], in_=xres[:, ts, :], func=AF.Square,
                                                               accum_out=st[:, k:k + 1]))
                a2 = ACT.op(lambda e, k=k: e.activation(out=st[:, 16 + k:17 + k], in_=st[:, k:k + 1], func=AF.Sqrt,
                                                        scale=1.0 / D, bias=epsc[:, 0:1]))
                DVE.wait(a2)
                d1 = DVE.op(lambda e, k=k: e.reciprocal(out=st[:, 32 + k:33 + k], in_=st[:, 16 + k:17 + k]),
                            nowait=True)
                slot = state["tpi"] % 2
                state["tpi"] += 1
                DVE.wait(state["xn_free"][slot])
                DVE.wait(xdep[ts])
                d2_ = DVE.op(lambda e, ts=ts, k=k, slot=slot: e.tensor_scalar(
                    out=xn[:, slot, :], in0=xres[:, ts, :], scalar1=st[:, 32 + k:33 + k], scalar2=None, op0=ALU.mult))
                stA[ts] = (slot, d2_)

            def stage_b(ts):
                slot, d2_ = stA[ts]
                bk = banks[6 + slot]
                bk.begin(PE)
                PE.wait(d2_)
                tpb = bk.ap.bitcast(BF16)
                for c in range(NDC):
                    p1 = PE.op(lambda e, c=c, slot=slot, tpb=tpb: e.transpose(
                        out=tpb[:, c * 128:(c + 1) * 128], in_=xn[:, slot, c * 128:(c + 1) * 128], identity=ident),
                        inc=(c == NDC - 1))
                state["xn_free"][slot] = p1
                DVE.wait(p1)
                DVE.wait(state.get("hT_free"))
                d3 = DVE.op(lambda e, ts=ts, tpb=tpb, gcol=gcol: e.tensor_tensor(
                    out=hT[:, :, ts * 128:(ts + 1) * 128], in0=tpb.rearrange("p (c t) -> p c t", c=NDC),
                    in1=prm[:, gcol:gcol + NDC].unsqueeze(2).broadcast_to([128, NDC, 128]), op=ALU.mult),
                    nowait=True)
                bk.rel = [d3]
                ready[ts] = d3

            return stage_a, stage_b, ready

        def norm_phase(n, gcol, xdep, xres):
            stage_a, stage_b, ready = make_norm(n, gcol, xdep, xres)
            stage_a(0)
            for ts in range(NS):
                if ts + 1 < NS:
                    stage_a(ts + 1)
                stage_b(ts)
            return ready

        def ffn_phase(wg_h, wu_h, wd_h, hready, xres, chunk_hook=None, after_gu_hook=None, passb_hook=None,
                      evac_hook=None):
            hid_ready = [None] * NFC
            sg_free = [None, None]
            gu = None
            for j in range(NFC):
                if j % 2 == 0:
                    gs, gt_, gld = wacquire(gate_src(wg_h, j // 2), v_c256)
                    us, ut_, uld = wacquire(gate_src(wu_h, j // 2), v_c256)
                gb = banks[j % 2]
                ub = banks[2 + j % 2]
                gb.begin(PE)
                ub.begin(PE)
                if j == 0:
                    for d_ in hready:
                        PE.wait(d_)
                PE.wait(gld)
                PE.wait(uld)
                lo = (j % 2) * 128
                for c in range(NDC):
                    pg = PE.op(lambda e, c=c, gb=gb, gt_=gt_, lo=lo: e.matmul(
                        gb.ap, lhsT=gt_[:, c, lo:lo + 128], rhs=hT[:, c, :], start=(c == 0), stop=(c == NDC - 1)),
                        inc=(c == NDC - 1))
                for c in range(NDC):
                    pu = PE.op(lambda e, c=c, ub=ub, ut_=ut_, lo=lo: e.matmul(
                        ub.ap, lhsT=ut_[:, c, lo:lo + 128], rhs=hT[:, c, :], start=(c == 0), stop=(c == NDC - 1)),
                        inc=(c == NDC - 1))
                if j % 2 == 1:
                    wrelease(gs, pg)
                    wrelease(us, pu)
                ACT.wait(pg)
                ACT.wait(sg_free[j % 2])
                a = ACT.op(lambda e, j=j, gb=gb: e.activation(out=sg[:, j % 2, :], in_=gb.ap, func=AF.Silu),
                           nowait=True)
                gb.rel = [a]
                DVE.wait(pu)
                DVE.wait(a)
                d_ = DVE.op(lambda e, j=j, ub=ub: e.tensor_tensor(out=hid[:, j, :], in0=sg[:, j % 2, :], in1=ub.ap,
                                                                 op=ALU.mult), nowait=True)
                ub.rel = [d_]
                sg_free[j % 2] = d_
                hid_ready[j] = d_
                if chunk_hook is not None:
                    chunk_hook(j)
            if after_gu_hook is not None:
                after_gu_hook()
            else:
                ACT.op(lambda e: e.activation(out=st[:, 44:45], in_=st[:, 45:46], func=AF.Sqrt))
            xdep = [None] * NS
            for half in range(2):
                bks = [banks[4 + t_] for t_ in range(NS)] if half == 0 else [banks[t_] for t_ in range(NS)]
                for b_ in bks:
                    b_.begin(PE)
                pend = [None] * NS
                for k in range(6):
                    if half == 1 and k == 4:
                        s4, w4, l4 = wacquire(rows_src(wd_h, 16, 4, half), v_c512)
                        s5, w5, l5 = wacquire(rows_src(wd_h, 20, 2, half), v_c512)
                        PE.wait(l4)
                        PE.wait(l5)
                        for fc in range(16, NFC):
                            PE.wait(hid_ready[fc])
                        for ts in range(NS):
                            for fc in range(16, NFC):
                                wt_, fl = (w4, fc - 16) if fc < 20 else (w5, fc - 20)
                                rel4 = (ts == NS - 1 and fc == 19)
                                pp_ = PE.op(lambda e, fc=fc, fl=fl, ts=ts, wt_=wt_, b_=bks[ts]: e.matmul(
                                    b_.ap, lhsT=hid[:, fc, ts * 128:(ts + 1) * 128], rhs=wt_[:, fl, :],
                                    start=False, stop=(fc == NFC - 1)), inc=(fc == NFC - 1 or rel4))
                                if fc == NFC - 1:
                                    pend[ts] = pp_
                                if rel4:
                                    wrelease(s4, pp_)
                        wrelease(s5, pend[NS - 1])
                        break
                    ncnk = 4 if k < 5 else 2
                    s_, wt_, ld_ = wacquire(rows_src(wd_h, 4 * k, ncnk, half), v_c512)
                    PE.wait(ld_)
                    for fl in range(ncnk):
                        fc = 4 * k + fl
                        PE.wait(hid_ready[fc])
                        for ts in range(NS):
                            last_unit_mm = (fl == ncnk - 1 and ts == NS - 1)
                            pp_ = PE.op(lambda e, fc=fc, fl=fl, ts=ts, wt_=wt_, b_=bks[ts]: e.matmul(
                                b_.ap, lhsT=hid[:, fc, ts * 128:(ts + 1) * 128], rhs=wt_[:, fl, :],
                                start=(fc == 0), stop=(fc == NFC - 1)),
                                inc=(fc == NFC - 1 or last_unit_mm))
                            if fc == NFC - 1:
                                pend[ts] = pp_
                            if last_unit_mm:
                                wrelease(s_, pp_)
                    if half == 1 and passb_hook is not None:
                        passb_hook(k)
                for ts in range(NS):
                    DVE.wait(pend[ts])
                    d_ = DVE.op(lambda e, ts=ts, half=half, b_=bks[ts]: e.scalar_tensor_tensor(
                        out=xres[:, ts, half * 512:(half + 1) * 512], in0=b_.ap, scalar=0.5,
                        in1=xres[:, ts, half * 512:(half + 1) * 512], op0=ALU.mult, op1=ALU.add), nowait=True)
                    bks[ts].rel = [d_]
                    xdep[ts] = d_
                    if half == 1 and evac_hook is not None:
                        evac_hook(ts, xdep)
            return xdep

        def mixer_phase(i, hready, xres, pre_attn_hook=None):
            kbase = (i % 2) * T
            vbase = (i % 2) * 4
            sq_free = [None, None]
            rr_free = [None, None]
            info = {}
            qk_last = None

            SEQ = [0, 4, 1, 5, 2, 6, 3, 7]
            POS = {fq: n for n, fq in enumerate(SEQ)}
            qk_dep = {}

            def proj(fq):
                pos = POS[fq]
                unit = fq // 2
                if ("w", unit) not in info:
                    info[("w", unit)] = wacquire(gate_src(win_h, unit), v_c256)
                s_, wt_, ld_ = info[("w", unit)]
                pb = banks[pos % 4]
                pb.begin(PE)
                if pos == 0:
                    for d_ in hready:
                        PE.wait(d_)
                PE.wait(ld_)
                lo = (fq % 2) * 128
                for c in range(NDC):
                    p = PE.op(lambda e, c=c, pb=pb, wt_=wt_, lo=lo: e.matmul(
                        pb.ap, lhsT=wt_[:, c, lo:lo + 128], rhs=hT[:, c, :], start=(c == 0), stop=(c == NDC - 1)),
                        inc=(c == NDC - 1))
                if fq % 2 == 1:
                    wrelease(s_, p)
                ACT.wait(p)
                ACT.wait(sq_free[pos % 2])
                a = ACT.op(lambda e, pos=pos, pb=pb: e.activation(out=sq[:, pos % 2, :], in_=pb.ap, func=AF.Square),
                           nowait=True)
                info[("sq", fq)] = a

            def ones(fq):
                nonlocal qk_last
                pos = POS[fq]
                pb = banks[pos % 4]
                sbk = banks[4 + pos % 2]
                sbk.begin(PE)
                PE.wait(info[("sq", fq)])
                p2 = PE.op(lambda e, pos=pos, sbk=sbk: e.matmul(sbk.ap, lhsT=bdiag, rhs=sq[:, pos % 2, :],
                                                                start=True, stop=True))
                sq_free[pos % 2] = p2
                ACT.wait(p2)
                ACT.wait(rr_free[pos % 2])
                a2 = ACT.op(lambda e, pos=pos, sbk=sbk: e.activation(out=rr[:, pos % 2, :], in_=sbk.ap, func=AF.Ln,
                                                                     scale=1.0 / 64, bias=epsc[:, 0:1]), nowait=True)
                sbk.rel = [a2]
                a3 = ACT.op(lambda e, pos=pos: e.activation(out=rr[:, pos % 2, :], in_=rr[:, pos % 2, :], func=AF.Exp,
                                                            scale=-0.5))
                DVE.wait(a3)
                if fq < 4:
                    dest = qn[:, fq, :]
                    gc = QG
                else:
                    dest = kT[:, fq - 4, kbase:kbase + T]
                    gc = KG
                d2_ = DVE.op(lambda e, pos=pos, pb=pb, dest=dest, gc=gc: e.scalar_tensor_tensor(
                    out=dest, in0=pb.ap, scalar=prm[:, gc:gc + 1], in1=rr[:, pos % 2, :], op0=ALU.mult, op1=ALU.mult),
                    nowait=True)
                pb.rel = [d2_]
                rr_free[pos % 2] = d2_
                qk_last = d2_
                qk_dep[fq] = d2_

            def run_M1():
                proj(SEQ[0])
                for n in range(1, 8):
                    proj(SEQ[n])
                    ones(SEQ[n - 1])
                ones(SEQ[7])

            mres = {}

            def run_M2():
                v_ready = None
                pend = [None] * NS
                for u in range(2):
                    s_, wt_, ld_ = wacquire(gate_src(win_h, 4 + u), v_c256)
                    PE.wait(ld_)
                    for ts in range(NS):
                        bk = banks[ts]
                        if u == 0:
                            bk.begin(PE)
                        for c in range(NDC):
                            lastmm = (c == NDC - 1)
                            p = PE.op(lambda e, c=c, ts=ts, u=u, bk=bk, wt_=wt_: e.matmul(
                                bk.ap[:, u * 256:(u + 1) * 256], lhsT=hT[:, c, ts * 128:(ts + 1) * 128], rhs=wt_[:, c, :],
                                start=(c == 0), stop=(c == NDC - 1)),
                                inc=(lastmm and (u == 1 or ts == NS - 1)))
                            if lastmm and u == 1:
                                pend[ts] = p
                        if ts == NS - 1:
                            wrelease(s_, p)
                for ts in range(NS):
                    ACT.wait(pend[ts])
                    a = ACT.op(lambda e, ts=ts: e.activation(
                        out=V[:, vbase + ts, :, 0:64], in_=banks[ts].ap.rearrange("p (h d) -> p h d", d=64), func=AF.Copy),
                        nowait=True)
                    banks[ts].rel = [a]
                    mres["v_ready"] = a

            u_ready = [None] * 4
            for fu in range(4):
                if fu % 2 == 0:
                    s_, wt_, ld_ = wacquire(gate_src(win_h, 6 + fu // 2), v_c256)
                bk = banks[4 + fu % 2]
                bk.begin(PE)
                if fu == 0:
                    for d_ in hready:
                        PE.wait(d_)
                PE.wait(ld_)
                lo = (fu % 2) * 128
                for c in range(NDC):
                    p = PE.op(lambda e, c=c, bk=bk, wt_=wt_, lo=lo: e.matmul(
                        bk.ap, lhsT=wt_[:, c, lo:lo + 128], rhs=hT[:, c, :], start=(c == 0), stop=(c == NDC - 1)),
                        inc=(c == NDC - 1))
                if fu % 2 == 1:
                    wrelease(s_, p)
                ACT.wait(p)
                a = ACT.op(lambda e, fu=fu, bk=bk: e.activation(out=uT[:, fu, 16:16 + T], in_=bk.ap, func=AF.Copy),
                           nowait=True)
                bk.rel = [a]
                u_ready[fu] = a

            W_ = 16 + T
            pool_ready = None
            dT_ready = [None] * 4
            for g in range(4):
                DVE.wait(u_ready[g])
                U = uT[:, g, :]
                A = ptmp[:, 0, :]
                B = ptmp[:, 1, :]
                dprev = DVE.op(lambda e, U=U, A=A: e.tensor_tensor(out=A[:, 1:W_], in0=U[:, 1:W_], in1=U[:, 0:W_ - 1],
                                                                   op=ALU.add))
                Sx = A
                if g >= 1:
                    DVE.wait(dprev)
                    dprev = DVE.op(lambda e, A=A, B=B: e.tensor_tensor(out=B[:, 3:W_], in0=A[:, 3:W_], in1=A[:, 1:W_ - 2],
                                                                       op=ALU.add))
                    Sx = B
                if g >= 2:
                    DVE.wait(dprev)
                    dprev = DVE.op(lambda e, A=A, B=B: e.tensor_tensor(out=A[:, 7:W_], in0=B[:, 7:W_], in1=B[:, 3:W_ - 4],
                                                                       op=ALU.add))
                    Sx = A
                if g >= 3:
                    DVE.wait(dprev)
                    dprev = DVE.op(lambda e, A=A, B=B: e.tensor_tensor(out=B[:, 15:W_], in0=A[:, 15:W_], in1=A[:, 7:W_ - 8],
                                                                       op=ALU.add))
                    Sx = B
                DVE.wait(dprev)
                dprev = DVE.op(lambda e, g=g, Sx=Sx, U=U: e.scalar_tensor_tensor(
                    out=dT[:, g, :], in0=Sx[:, 16:W_], scalar=1.0 / POOL_W[g], in1=U[:, 16:W_],
                    op0=ALU.mult, op1=ALU.subtract))
                if i == 0:
                    DVE.wait(dprev)
                    dprev = DVE.op(lambda e, g=g, Sx=Sx: e.tensor_tensor(out=ptmp2[:], in0=Sx[:, 16:32], in1=invc[:, g, :],
                                                                         op=ALU.mult))
                    DVE.wait(dprev)
                    dprev = DVE.op(lambda e, g=g, U=U: e.tensor_tensor(out=dT[:, g, 0:16], in0=ptmp2[:], in1=U[:, 16:32],
                                                                       op=ALU.subtract))
                dT_ready[g] = dprev
            DVE.wait(dprev)
            DVE.op(lambda e: e.tensor_copy(out=uT[:, :, 0:16], in_=uT[:, :, T:T + 16]))

            run_M2()
            v_ready = mres["v_ready"]
            for g in range(4):
                bk = banks[(6, 7, 4, 5)[g]]
                bk.begin(PE)
                PE.wait(dT_ready[g])
                p = PE.op(lambda e, g=g, bk=bk: e.matmul(bk.ap, lhsT=poolw[:, g, :], rhs=dT[:, g, :], start=True, stop=True))
                DVE.wait(p)
                a = DVE.op(lambda e, g=g, bk=bk: e.tensor_scalar(
                    out=apT[:, 4 + g, :], in0=bk.ap, scalar1=prm[:, PSC + g:PSC + g + 1], scalar2=None, op0=ALU.mult),
                    nowait=True)
                bk.rel = [a]
                pool_ready = a
            run_M1()
            qk_ready = qk_last

            hps = [(pp, h) for pp in range(4) for h in range(8)]
            expS_free = [None, None, None]
            PT_free = [None, None, None]
            SPAIR = [(0, 1), (2, 3), (6, 7)]
            atok_free = [None, None]
            pt_ready = {}
            att = {"last": None, "ready": [None] * 4}

            def groups(pp):
                G = 4 * i + pp
                return [(gi, (G - 4 + gi) % 8) for gi in range(5) if G - 4 + gi >= 0]

            def emit_S(idx):
                pp, h = hps[idx]
                kc, pr = h // 2, (h % 2) * 64
                gl = groups(pp)
                gA = [g_ for g_ in gl if g_[0] < 4]
                SA = banks[SPAIR[idx % 3][0]]
                SB = banks[SPAIR[idx % 3][1]]
                PE.wait(qk_dep[kc])
                PE.wait(qk_dep[4 + kc])
                if idx == 0:
                    PE.wait(v_ready)
                pA = None
                if gA:
                    SA.begin(PE)
                    for n_, (gi, sl) in enumerate(gA):
                        pA = PE.op(lambda e, gi=gi, sl=sl, SA=SA, kc=kc, pr=pr, pp=pp: e.matmul(
                            SA.ap[:, gi * 128:(gi + 1) * 128], lhsT=kT[pr:pr + 64, kc, sl * 128:(sl + 1) * 128],
                            rhs=qn[pr:pr + 64, kc, pp * 128:(pp + 1) * 128], start=True, stop=True),
                            inc=(n_ == len(gA) - 1))
                SB.begin(PE)
                gi, sl = gl[-1]
                pB = PE.op(lambda e, sl=sl, SB=SB, kc=kc, pr=pr, pp=pp: e.matmul(
                    SB.ap[:, 0:128], lhsT=kT[pr:pr + 64, kc, sl * 128:(sl + 1) * 128],
                    rhs=qn[pr:pr + 64, kc, pp * 128:(pp + 1) * 128], start=True, stop=True))
                ACT.wait(expS_free[idx % 2])
                g0 = gl[0][0]
                if gA:
                    ACT.wait(pA)
                    aA = ACT.op(lambda e, idx=idx, SA=SA, g0=g0: e.activation(
                        out=expS[:, idx % 2, g0 * 128:512], in_=SA.ap[:, g0 * 128:512], func=AF.Exp, scale=0.125),
                        nowait=True)
                    SA.rel = [aA]
                ACT.wait(pB)
                aB = ACT.op(lambda e, idx=idx, SB=SB: e.activation(
                    out=expS[:, idx % 2, 512:640], in_=SB.ap[:, 0:128], func=AF.Exp, scale=0.125), nowait=True)
                SB.rel = [aB]
                DVE.wait(PT_free[idx % 3])
                DVE.wait(aB)
                d_ = DVE.op(lambda e, idx=idx, h=h, g0=g0: e.tensor_tensor(
                    out=PT[:, idx % 3, g0 * 128:640], in0=expS[:, idx % 2, g0 * 128:640],
                    in1=etab[:, h, g0:5, :].rearrange("p g q -> p (g q)"), op=ALU.mult), nowait=True)
                expS_free[idx % 2] = d_
                pt_ready[idx] = d_

            def emit_PV(idx):
                pp, h = hps[idx]
                gl = groups(pp)
                OB = banks[4 + h // 4]
                if h % 4 == 0:
                    OB.begin(PE)
                PE.wait(pt_ready[idx])
                c0 = (h % 4) * 65
                for n_, (gi, sl) in enumerate(gl):
                    pO = PE.op(lambda e, gi=gi, sl=sl, OB=OB, c0=c0, idx=idx, h=h, n_=n_: e.matmul(
                        OB.ap[:, c0:c0 + 65], lhsT=PT[:, idx % 3, gi * 128:(gi + 1) * 128], rhs=V[:, sl, h, :],
                        start=(n_ == 0), stop=(n_ == len(gl) - 1)), inc=(n_ == len(gl) - 1))
                PT_free[idx % 3] = pO
                if h % 4 == 3:
                    hg = h // 4
                    rk = (pp % 2) * 2 + hg
                    obv = OB.ap[:, 0:260].rearrange("p (h e) -> p h e", e=65)
                    DVE.wait(pO)
                    if hg == 0:
                        DVE.wait(atok_free[pp % 2])
                    d1 = DVE.op(lambda e, rk=rk, obv=obv: e.reciprocal(out=rec[:, rk * 4:rk * 4 + 4].unsqueeze(2),
                                                                       in_=obv[:, :, 64:65]), nowait=True)
                    DVE.wait(d1)
                    d2_ = DVE.op(lambda e, rk=rk, obv=obv, pp=pp, hg=hg: e.tensor_tensor(
                        out=atok[:, pp % 2, hg * 256:(hg + 1) * 256].rearrange("p (h d) -> p h d", d=64),
                        in0=obv[:, :, 0:64],
                        in1=rec[:, rk * 4:rk * 4 + 4].unsqueeze(2).broadcast_to([128, 4, 64]), op=ALU.mult))
                    OB.rel = [d2_]
                    if hg == 1:
                        att["pending"] = (idx, pp, d2_)

            def emit_TP(pp, d2_):
                tb = banks[5]
                tb.begin(PE)
                PE.wait(d2_)
                tpb = tb.ap.bitcast(BF16)
                for fc in range(4):
                    pT_ = PE.op(lambda e, fc=fc, pp=pp, tpb=tpb: e.transpose(
                        out=tpb[:, fc * 128:(fc + 1) * 128], in_=atok[:, pp % 2, fc * 128:(fc + 1) * 128],
                        identity=ident), inc=(fc == 3))
                atok_free[pp % 2] = pT_
                ACT.wait(pT_)
                a = ACT.op(lambda e, pp=pp, tpb=tpb: e.activation(
                    out=apT[:, 0:4, pp * 128:(pp + 1) * 128],
                    in_=tpb[:, 0:512].rearrange("p (c t) -> p c t", c=4), func=AF.Copy), nowait=True)
                tb.rel = [a]
                att["last"] = a
                att["ready"][pp] = a

            if pre_attn_hook is not None:
                pre_attn_hook(qk_ready)
            emit_S(0)
            emit_S(1)
            for idx in range(len(hps)):
                if idx + 2 < len(hps):
                    emit_S(idx + 2)
                emit_PV(idx)
                pend_tp = att.get("pending")
                if pend_tp is not None and (idx >= pend_tp[0] + 2 or idx == len(hps) - 1):
                    emit_TP(pend_tp[1], pend_tp[2])
                    att["pending"] = None
            attn_ready = att["last"]
            ACT.op(lambda e: e.activation(out=st[:, 44:45], in_=st[:, 45:46], func=AF.Sqrt))

            xdep = [None] * NS
            for half in range(2):
                bks = [banks[t_] for t_ in range(NS)] if half == 0 else [banks[4 + t_] for t_ in range(NS)]
                for b_ in bks:
                    b_.begin(PE)
                pend = [None] * NS
                s0, w0, l0 = wacquire(rows_src(wout_h, 0, 4, half), v_c512)
                s1, w1, l1 = wacquire(rows_src(wout_h, 4, 4, half), v_c512)
                PE.wait(l0)
                PE.wait(l1)
                PE.wait(pool_ready)
                for ts in range(NS):
                    PE.wait(att["ready"][ts])
                    for fc in range(8):
                        wt_, fl = (w0, fc) if fc < 4 else (w1, fc - 4)
                        rel0 = (ts == NS - 1 and fc == 3)
                        pp_ = PE.op(lambda e, fc=fc, fl=fl, ts=ts, wt_=wt_, b_=bks[ts]: e.matmul(
                            b_.ap, lhsT=apT[:, fc, ts * 128:(ts + 1) * 128], rhs=wt_[:, fl, :],
                            start=(fc == 0), stop=(fc == 7)), inc=(fc == 7 or rel0))
                        if fc == 7:
                            pend[ts] = pp_
                        if rel0:
                            wrelease(s0, pp_)
                wrelease(s1, pend[NS - 1])
                for ts in range(NS):
                    DVE.wait(pend[ts])
                    d_ = DVE.op(lambda e, ts=ts, half=half, b_=bks[ts]: e.tensor_tensor(
                        out=xres[:, ts, half * 512:(half + 1) * 512], in0=b_.ap,
                        in1=xres[:, ts, half * 512:(half + 1) * 512], op=ALU.add), nowait=True)
                    bks[ts].rel = [d_]
                    xdep[ts] = d_
            return xdep

        ost_cnt = [0, 0, 0, 0]
        pending_stores = []
        n4_done = {}

        def xload(i):
            xb = i % 2
            if i - 2 in n4_done:
                SP.wait(n4_done[i - 2][0])
                SP.wait(n4_done[i - 2][1])
            return SP.dma(lambda e, i=i, xb=xb: e.dma_start(
                out=xres2[:, xb], in_=x_h[i * T:(i + 1) * T, :].rearrange("(s p) d -> p s d", p=128)), xld[xb])

        n4_act_dep = {}
        n4_sq_dep = {}

        def n4_act(i, ts, xdep, with_sqrt=True):
            xr = xres2[:, i % 2]
            k = 12 + ts
            ACT.wait(xdep[ts])
            a1 = ACT.op(lambda e, ts=ts, k=k, xr=xr: e.activation(out=junk[:], in_=xr[:, ts, :], func=AF.Square,
                                                                  accum_out=st[:, k:k + 1]))
            n4_sq_dep[(i, ts)] = a1
            if with_sqrt:
                a2 = ACT.op(lambda e, k=k: e.activation(out=st[:, 16 + k:17 + k], in_=st[:, k:k + 1], func=AF.Sqrt,
                                                        scale=1.0 / D, bias=epsc[:, 0:1]))
                n4_act_dep[(i, ts)] = a2

        def n4_sqrt_all(i):
            a2 = ACT.op(lambda e: e.activation(out=st[:, 28:32], in_=st[:, 12:16], func=AF.Sqrt,
                                               scale=1.0 / D, bias=epsc[:, 0:1]))
            for ts in range(NS):
                n4_act_dep[(i, ts)] = a2

        def n4_dve(i, ts, xdep):
            xr = xres2[:, i % 2]
            k = 12 + ts
            a2 = n4_act_dep[(i, ts)]
            DVE.wait(c_all)
            DVE.wait(a2)
            d1 = DVE.op(lambda e, k=k: e.reciprocal(out=st[:, 32 + k:33 + k], in_=st[:, 16 + k:17 + k]))
            sl = ts
            DVE.wait((ost[sl], ost_cnt[sl]))
            DVE.wait(xdep[ts])
            d2_ = DVE.op(lambda e, ts=ts, k=k, sl=sl, xr=xr: e.scalar_tensor_tensor(
                out=ostage[:, sl, :], in0=xr[:, ts, :], scalar=st[:, 32 + k:33 + k], in1=gfin[:],
                op0=ALU.mult, op1=ALU.mult))
            ost_cnt[sl] += 16
            pending_stores.append((i, ts, sl, d2_))
            n4_done[i] = (a2, d2_)
            if i + 1 == NT:
                flush_stores()

        def flush_stores():
            while pending_stores:
                i, ts, sl, d2_ = pending_stores.pop(0)
                SP.wait(d2_)
                SP.dma(lambda e, i=i, ts=ts, sl=sl: e.dma_start(
                    out=y_h[i * T + ts * 128:i * T + (ts + 1) * 128, :], in_=ostage[:, sl, :]), ost[sl])

        def n4_piece(i, ts, xdep):
            n4_act(i, ts, xdep)
            n4_dve(i, ts, xdep)

        xl = {0: xl0}
        hready = None
        final_xdep = {}
        for i in range(NT):
            xr = xres2[:, i % 2]
            if i == 0:
                hready = norm_phase(0, G1, [xl[0]] * NS, xr)
            if i > 0:
                def hook1(j, i=i):
                    if j % 2 == 1 and j < 2 * NS:
                        n4_act(i - 1, (j - 1) // 2, final_xdep[i - 1], with_sqrt=False)
                        if j == 2 * NS - 1:
                            n4_sqrt_all(i - 1)
                    if 2 * NS <= j < 3 * NS:
                        n4_dve(i - 1, j - 2 * NS, final_xdep[i - 1])
            else:
                def hook1(j):
                    if 2 <= j < 10:
                        etab_exp(j - 2)
            xdep = ffn_phase(wg1_h, wu1_h, wd1_h, hready, xr, chunk_hook=hook1)
            hready = norm_phase(1, GM, xdep, xr)
            if i == 0:
                DVE.wait(c_all)
                DVE.wait(etab_state["ready"])
            pah = None
            if i + 1 < NT:
                def pah(dep, i=i):
                    SP.wait(dep)
                    xl[i + 1] = xload(i + 1)
                    flush_stores()
            xdep = mixer_phase(i, hready, xr, pre_attn_hook=pah)
            hready = norm_phase(2, G2, xdep, xr)
            agu = pbh = None
            nxt = {}
            if i + 1 < NT:
                sa, sb_, rdy = make_norm(0, G1, [xl[i + 1]] * NS, xres2[:, (i + 1) % 2])
                nxt["ready"] = rdy

                def agu(sa=sa):
                    sa(0)
                    sa(1)

                def pbh(k, sa=sa, sb_=sb_):
                    if k < NS:
                        sb_(k)
                        if k + 2 < NS:
                            sa(k + 2)
            evh = None
            if i + 1 == NT:
                def evh(ts, xdep_, i=i):
                    DVE.wait(c_all)
                    n4_piece(i, ts, xdep_)
            xdep = ffn_phase(wg2_h, wu2_h, wd2_h, hready, xr, after_gu_hook=agu, passb_hook=pbh, evac_hook=evh)
            final_xdep[i] = xdep
            if i + 1 < NT:
                hready = nxt["ready"]
        flush_stores()
        for k in range(4):
            SP.wait((ost[k], ost_cnt[k]))

        @block.sync
        def _(e):
            SP.replay(e)

        @block.gpsimd
        def _(e):
            POOL.replay(e)

        @block.scalar
        def _(e):
            ACT.replay(e)

        @block.vector
        def _(e):
            DVE.replay(e)

        @block.tensor
        def _(e):
            PE.replay(e)
    return nc


def host_consts(q_norm, k_norm, rel_bias, pool_w, pool_scale, ffn1_norm, mix_norm, ffn2_norm, final_norm):
    f32 = np.float32
    prm = np.zeros((128, 32), f32)
    prm[:, G1:G1 + 8] = np.asarray(ffn1_norm, f32).reshape(8, 128).T
    prm[:, GM:GM + 8] = np.asarray(mix_norm, f32).reshape(8, 128).T
    prm[:, G2:G2 + 8] = np.asarray(ffn2_norm, f32).reshape(8, 128).T
    prm[:, QG] = np.tile(np.asarray(q_norm, f32), 2)
    prm[:, KG] = np.tile(np.asarray(k_norm, f32), 2)
    prm[:, PSC:PSC + 4] = np.asarray(pool_scale, f32).reshape(4, 128).T
    gfin = np.ascontiguousarray(np.broadcast_to(np.asarray(final_norm, f32)[None, :], (128, D)))
    cb = np.zeros((128, 256), f32)
    cb[:, 0:128] = np.eye(128, dtype=f32)
    blk = np.arange(128) // 64
    cb[:, 128:256] = (blk[:, None] == blk[None, :]).astype(f32)
    invc = np.zeros((128, 4, 16), f32)
    for g, w in enumerate(POOL_W):
        invc[:, g, :] = (1.0 / np.minimum(np.arange(16) + 1, w)).astype(f32)[None, :]
    invc = invc.reshape(128, 64)
    kj = np.arange(640)[:, None]
    qi = np.arange(128)[None, :]
    idx = np.clip(512 + qi - kj, -128, 128) + 128
    dchunk = kj // 64 - qi // 64
    valid = (dchunk >= 0) & (dchunk <= 8)
    rb = np.asarray(rel_bias, f32)
    bt = rb[:, idx]
    bt = np.where(valid[None], bt, f32(-200.0))
    bt = bt.reshape(8, 5, 128, 128).transpose(2, 0, 1, 3)
    biasT = np.ascontiguousarray(bt).reshape(128, 8 * 5 * 128).astype(f32)
    pw = np.asarray(pool_w, f32).transpose(1, 0, 2).reshape(128, 512)
    return dict(prm=prm, gfin=gfin, cb=cb, invc=invc, biasT=biasT, poolw=np.ascontiguousarray(pw))


_NC_CACHE = {}


def make_in_maps(inputs, S):
    f32 = np.float32
    c = host_consts(inputs["q_norm"][0], inputs["k_norm"][0], inputs["rel_bias"][0], inputs["pool_w"][0],
                    inputs["pool_scale"][0], inputs["ffn1_norm"][0], inputs["mix_norm"][0], inputs["ffn2_norm"][0],
                    inputs["final_norm"][0])
    shared = dict(
        wg1=np.ascontiguousarray(inputs["ffn1_w_gate"][0], f32), wu1=np.ascontiguousarray(inputs["ffn1_w_up"][0], f32),
        wd1=np.ascontiguousarray(inputs["ffn1_w_down"][0], f32), wg2=np.ascontiguousarray(inputs["ffn2_w_gate"][0], f32),
        wu2=np.ascontiguousarray(inputs["ffn2_w_up"][0], f32), wd2=np.ascontiguousarray(inputs["ffn2_w_down"][0], f32),
        win=np.ascontiguousarray(inputs["w_in"][0], f32), wout=np.ascontiguousarray(inputs["w_out"][0], f32), **c)
    x = np.asarray(inputs["x"], f32)
    return [dict(x=np.ascontiguousarray(x[b, :S]), **shared) for b in range(x.shape[0])]


def kernel(**inputs):
    inputs = {k: np.asarray(v) for k, v in inputs.items()}
    B, S, _ = inputs["x"].shape
    if S not in _NC_CACHE:
        _NC_CACHE[S] = build_program(S)
    nc = _NC_CACHE[S]
    in_maps = make_in_maps(inputs, S)
    res = run_bass_kernel_spmd(nc, in_maps, core_ids=list(range(B)))
    return np.stack([np.asarray(r["y"], np.float32) for r in res.results], axis=0)
```

```python
import numpy as np
from contextlib import ExitStack
import concourse.bass as bass
import concourse.mybir as mybir
from concourse.bass_utils import run_bass_kernel_spmd

F32 = mybir.dt.float32
BF16 = mybir.dt.bfloat16
ALU = mybir.AluOpType
AF = mybir.ActivationFunctionType

D = 1024
DFF = 2816
NFC = DFF // 128
NDC = D // 128
T = 512
NS = T // 128
NW = 8
EPS = 1e-6
POOL_W = (2, 4, 8, 16)
G1, GM, G2, QG, KG, PSC = 0, 8, 16, 24, 25, 26


class Ev:
    def __init__(self, nc, es, name):
        self.sem = es.enter_context(nc.semaphore(name))
        self.n = 0
        self.name = name


class Q:
    def __init__(self, nc, es, name, serial=False):
        self.name = name
        self.ops = []
        self.waited = {}
        self.ev = Ev(nc, es, "ev_" + name)
        self.serial = serial

    def wait(self, dep):
        if dep is None:
            return
        ev, val = dep
        if val <= 0:
            return
        if self.waited.get(ev.name, 0) >= val:
            return
        self.waited[ev.name] = val
        self.ops.append(("w", ev, val))

    def op(self, fn, inc=True, nowait=False):
        if self.serial and not nowait:
            self.wait((self.ev, self.ev.n))
        if inc:
            self.ev.n += 1
            self.ops.append(("o", fn, self.ev, 1))
            return (self.ev, self.ev.n)
        self.ops.append(("o", fn, None, 0))
        return None

    def dma(self, fn, ev):
        ev.n += 16
        self.ops.append(("o", fn, ev, 16))
        return (ev, ev.n)

    def replay(self, eng):
        for o in self.ops:
            if o[0] == "w":
                eng.wait_ge(o[1].sem, o[2])
            else:
                ins = o[1](eng)
                if o[2] is not None:
                    ins.then_inc(o[2].sem, o[3])


class Bank:
    def __init__(self, ap):
        self.ap = ap
        self.rel = []

    def begin(self, PE):
        for d in self.rel:
            PE.wait(d)
        self.rel = []


def build_program(S):
    NT = S // T
    nc = bass.Bass("TRN2", target_bir_lowering=False)
    dt = nc.dram_tensor
    x_h = dt("x", [S, D], F32, kind="ExternalInput").ap()
    wg1_h = dt("wg1", [D, DFF], F32, kind="ExternalInput").ap()
    wu1_h = dt("wu1", [D, DFF], F32, kind="ExternalInput").ap()
    wd1_h = dt("wd1", [DFF, D], F32, kind="ExternalInput").ap()
    wg2_h = dt("wg2", [D, DFF], F32, kind="ExternalInput").ap()
    wu2_h = dt("wu2", [D, DFF], F32, kind="ExternalInput").ap()
    wd2_h = dt("wd2", [DFF, D], F32, kind="ExternalInput").ap()
    win_h = dt("win", [D, 2048], F32, kind="ExternalInput").ap()
    wout_h = dt("wout", [D, D], F32, kind="ExternalInput").ap()
    poolw_h = dt("poolw", [128, 512], F32, kind="ExternalInput").ap()
    prm_h = dt("prm", [128, 32], F32, kind="ExternalInput").ap()
    gfin_h = dt("gfin", [128, D], F32, kind="ExternalInput").ap()
    cb_h = dt("cb", [128, 256], F32, kind="ExternalInput").ap()
    invc_h = dt("invc", [128, 64], F32, kind="ExternalInput").ap()
    bias_h = dt("biasT", [128, 8 * 5 * 128], F32, kind="ExternalInput").ap()
    y_h = dt("y", [S, D], F32, kind="ExternalOutput").ap()

    with ExitStack() as es:
        E = es.enter_context
        sb = lambda name, shape, dtype: E(nc.sbuf_tensor(name, shape, dtype))
        xres2 = sb("xres", [128, 2, NS, D], F32)
        junk = sb("junk", [128, D], BF16)
        st = sb("st", [128, 48], F32)
        xn = sb("xn", [128, 2, D], BF16)
        hT = sb("hT", [128, NDC, T], BF16)
        hid = sb("hid", [128, NFC, T], BF16)
        sg = sb("sg", [128, 2, T], F32)
        wr = sb("wr", [128, NW, 2048], BF16)
        prm = sb("prm_s", [128, 32], F32)
        epsc = sb("epsc", [128, 2], F32)
        gfin = sb("gfin_s", [128, D], F32)
        cb = sb("cb_s", [128, 256], BF16)
        poolw = sb("poolw_s", [128, 4, 128], BF16)
        invc = sb("invc_s", [128, 4, 16], F32)
        etab = sb("etab", [128, 8, 5, 128], F32)
        qn = sb("qn", [128, 4, T], BF16)
        kT = sb("kT", [128, 4, 2 * T], BF16)
        V = sb("V", [128, 8, 8, 65], BF16)
        uT = sb("uT", [128, 4, 16 + T], F32)
        ptmp = sb("ptmp", [128, 2, 16 + T], F32)
        ptmp2 = sb("ptmp2", [128, 16], F32)
        dT = sb("dT", [128, 4, T], BF16)
        apT = sb("apT", [128, 8, T], BF16)
        atok = sb("atok", [128, 2, 512], BF16)
        expS = sb("expS", [128, 2, 640], F32)
        PT = sb("PT", [128, 3, 640], BF16)
        sq = sb("sq", [128, 2, T], BF16)
        rr = sb("rr", [128, 2, T], F32)
        rec = sb("rec", [128, 16], F32)
        ostage = sb("ostage", [128, 4, D], F32)
        ps = E(nc.psum_tensor("ps", [128, 4096], F32))

        PE = Q(nc, es, "pe")
        ACT = Q(nc, es, "act", serial=True)
        DVE = Q(nc, es, "dve", serial=True)
        POOL = Q(nc, es, "pool")
        SP = Q(nc, es, "sp")
        xld = [Ev(nc, es, "xld0"), Ev(nc, es, "xld1")]
        cld = Ev(nc, es, "cld")
        cld2 = Ev(nc, es, "cld2")
        cld3 = Ev(nc, es, "cld3")
        ost = [Ev(nc, es, f"ost{k}") for k in range(4)]
        wld = [Ev(nc, es, f"wld{s}") for s in range(NW)]
        block = E(nc.Block())

        banks = [Bank(ps[:, b * 512:(b + 1) * 512]) for b in range(8)]
        ident = cb[:, 0:128]
        bdiag = cb[:, 128:256]

        xl0 = SP.dma(lambda e: e.dma_start(
            out=xres2[:, 0], in_=x_h[0:T, :].rearrange("(s p) d -> p s d", p=128)), xld[0])
        c_prm = SP.dma(lambda e: e.dma_start(out=prm[:], in_=prm_h[:, :]), cld)
        d_c = []
        d_c.append(SP.dma(lambda e: e.dma_start(out=gfin[:], in_=gfin_h[:, :]), cld3))
        d_c.append(SP.dma(lambda e: e.dma_start(out=invc[:].rearrange("p g t -> p (g t)"), in_=invc_h[:, :]), cld3))
        d_c.append(SP.dma(lambda e: e.dma_start(out=etab[:].rearrange("p h g q -> p (h g q)"), in_=bias_h[:, :]), cld3))
        c_all = d_c[-1]
        d2 = POOL.dma(lambda e: e.dma_start(out=cb[:], in_=cb_h[:, :]), cld2)
        d2 = POOL.dma(lambda e: e.dma_start(out=poolw[:].rearrange("p g d -> p (g d)"), in_=poolw_h[:, :]), cld2)
        c2_all = d2

        m0 = DVE.op(lambda e: e.memset(epsc[:], EPS))
        m1 = DVE.op(lambda e: e.memset(V[:, :, :, 64:65], 1.0))
        m2 = DVE.op(lambda e: e.memset(uT[:, :, 0:16], 0.0))
        m3 = DVE.op(lambda e: e.memset(st[:], 1.0))
        setup_dve = m3
        etab_state = {"ready": None}

        def etab_exp(h):
            ACT.wait(c_all)
            etab_state["ready"] = ACT.op(lambda e, h=h: e.activation(
                out=etab[:, h].rearrange("p g q -> p (g q)"), in_=etab[:, h].rearrange("p g q -> p (g q)"),
                func=AF.Exp))
        ACT.wait(setup_dve)
        ACT.op(lambda e: e.activation(out=st[:, 40:41], in_=st[:, 41:42], func=AF.Square))
        PE.wait(c2_all)
        DVE.wait(c_prm)

        wstate = {"n": 0, "free": [None] * NW, "loads": [0] * NW}

        def wacquire(src_ap, view):
            u = wstate["n"]
            wstate["n"] += 1
            s = u % NW
            POOL.wait(wstate["free"][s])
            n_el = 1
            for d_ in src_ap.shape[1:]:
                n_el *= d_
            dst = view(wr[:, s, 0:n_el])
            dep = POOL.dma(lambda e, dst=dst, src=src_ap: e.dma_start(out=dst, in_=src), wld[s])
            return s, dst, dep

        def wrelease(s, dep):
            wstate["free"][s] = dep

        def gate_src(w_h, p):
            return w_h.rearrange("(c p) f -> p c f", p=128)[:, :, 256 * p:256 * p + 256]

        def rows_src(w_h, c0, ncnk, half):
            return w_h.rearrange("(c p) d -> p c d", p=128)[:, c0:c0 + ncnk, half * 512:half * 512 + 512]

        v_c256 = lambda ap: ap.rearrange("p (c f) -> p c f", f=256)
        v_c512 = lambda ap: ap.rearrange("p (c f) -> p c f", f=512)

        state = {"xn_free": [None, None], "tpi": 0}

        def make_norm(n, gcol, xdep, xres):
            ready = [None] * NS
            stA = {}

            def stage_a(ts):
                k = n * 4 + ts
                ACT.wait(xdep[ts])
                a1 = ACT.op(lambda e, ts=ts, k=k: e.activation(out=junk[:], in_=xres[:, ts, :], func=AF.Square,
                                                               accum_out=st[:, k:k + 1]))
                a2 = ACT.op(lambda e, k=k: e.activation(out=st[:, 16 + k:17 + k], in_=st[:, k:k + 1], func=AF.Sqrt,
                                                        scale=1.0 / D, bias=epsc[:, 0:1]))
                DVE.wait(a2)
                d1 = DVE.op(lambda e, k=k: e.reciprocal(out=st[:, 32 + k:33 + k], in_=st[:, 16 + k:17 + k]),
                            nowait=True)
                slot = state["tpi"] % 2
                state["tpi"] += 1
                DVE.wait(state["xn_free"][slot])
                DVE.wait(xdep[ts])
                d2_ = DVE.op(lambda e, ts=ts, k=k, slot=slot: e.tensor_scalar(
                    out=xn[:, slot, :], in0=xres[:, ts, :], scalar1=st[:, 32 + k:33 + k], scalar2=None, op0=ALU.mult))
                stA[ts] = (slot, d2_)

            def stage_b(ts):
                slot, d2_ = stA[ts]
                bk = banks[6 + slot]
                bk.begin(PE)
                PE.wait(d2_)
                tpb = bk.ap.bitcast(BF16)
                for c in range(NDC):
                    p1 = PE.op(lambda e, c=c, slot=slot, tpb=tpb: e.transpose(
                        out=tpb[:, c * 128:(c + 1) * 128], in_=xn[:, slot, c * 128:(c + 1) * 128], identity=ident),
                        inc=(c == NDC - 1))
                state["xn_free"][slot] = p1
                DVE.wait(p1)
                DVE.wait(state.get("hT_free"))
                d3 = DVE.op(lambda e, ts=ts, tpb=tpb, gcol=gcol: e.tensor_tensor(
                    out=hT[:, :, ts * 128:(ts + 1) * 128], in0=tpb.rearrange("p (c t) -> p c t", c=NDC),
                    in1=prm[:, gcol:gcol + NDC].unsqueeze(2).broadcast_to([128, NDC, 128]), op=ALU.mult),
                    nowait=True)
                bk.rel = [d3]
                ready[ts] = d3

            return stage_a, stage_b, ready

        def norm_phase(n, gcol, xdep, xres):
            stage_a, stage_b, ready = make_norm(n, gcol, xdep, xres)
            stage_a(0)
            for ts in range(NS):
                if ts + 1 < NS:
                    stage_a(ts + 1)
                stage_b(ts)
            return ready

        def ffn_phase(wg_h, wu_h, wd_h, hready, xres, chunk_hook=None, after_gu_hook=None, passb_hook=None,
                      evac_hook=None):
            hid_ready = [None] * NFC
            sg_free = [None, None]
            gu = None
            for j in range(NFC):
                if j % 2 == 0:
                    gs, gt_, gld = wacquire(gate_src(wg_h, j // 2), v_c256)
                    us, ut_, uld = wacquire(gate_src(wu_h, j // 2), v_c256)
                gb = banks[j % 2]
                ub = banks[2 + j % 2]
                gb.begin(PE)
                ub.begin(PE)
                if j == 0:
                    for d_ in hready:
                        PE.wait(d_)
                PE.wait(gld)
                PE.wait(uld)
                lo = (j % 2) * 128
                for c in range(NDC):
                    pg = PE.op(lambda e, c=c, gb=gb, gt_=gt_, lo=lo: e.matmul(
                        gb.ap, lhsT=gt_[:, c, lo:lo + 128], rhs=hT[:, c, :], start=(c == 0), stop=(c == NDC - 1)),
                        inc=(c == NDC - 1))
                for c in range(NDC):
                    pu = PE.op(lambda e, c=c, ub=ub, ut_=ut_, lo=lo: e.matmul(
                        ub.ap, lhsT=ut_[:, c, lo:lo + 128], rhs=hT[:, c, :], start=(c == 0), stop=(c == NDC - 1)),
                        inc=(c == NDC - 1))
                if j % 2 == 1:
                    wrelease(gs, pg)
                    wrelease(us, pu)
                ACT.wait(pg)
                ACT.wait(sg_free[j % 2])
                a = ACT.op(lambda e, j=j, gb=gb: e.activation(out=sg[:, j % 2, :], in_=gb.ap, func=AF.Silu),
                           nowait=True)
                gb.rel = [a]
                DVE.wait(pu)
                DVE.wait(a)
                d_ = DVE.op(lambda e, j=j, ub=ub: e.tensor_tensor(out=hid[:, j, :], in0=sg[:, j % 2, :], in1=ub.ap,
                                                                 op=ALU.mult), nowait=True)
                ub.rel = [d_]
                sg_free[j % 2] = d_
                hid_ready[j] = d_
                if chunk_hook is not None:
                    chunk_hook(j)
            if after_gu_hook is not None:
                after_gu_hook()
            else:
                ACT.op(lambda e: e.activation(out=st[:, 44:45], in_=st[:, 45:46], func=AF.Sqrt))
            xdep = [None] * NS
            for half in range(2):
                bks = [banks[4 + t_] for t_ in range(NS)] if half == 0 else [banks[t_] for t_ in range(NS)]
                for b_ in bks:
                    b_.begin(PE)
                pend = [None] * NS
                for k in range(6):
                    if half == 1 and k == 4:
                        s4, w4, l4 = wacquire(rows_src(wd_h, 16, 4, half), v_c512)
                        s5, w5, l5 = wacquire(rows_src(wd_h, 20, 2, half), v_c512)
                        PE.wait(l4)
                        PE.wait(l5)
                        for fc in range(16, NFC):
                            PE.wait(hid_ready[fc])
                        for ts in range(NS):
                            for fc in range(16, NFC):
                                wt_, fl = (w4, fc - 16) if fc < 20 else (w5, fc - 20)
                                rel4 = (ts == NS - 1 and fc == 19)
                                pp_ = PE.op(lambda e, fc=fc, fl=fl, ts=ts, wt_=wt_, b_=bks[ts]: e.matmul(
                                    b_.ap, lhsT=hid[:, fc, ts * 128:(ts + 1) * 128], rhs=wt_[:, fl, :],
                                    start=False, stop=(fc == NFC - 1)), inc=(fc == NFC - 1 or rel4))
                                if fc == NFC - 1:
                                    pend[ts] = pp_
                                if rel4:
                                    wrelease(s4, pp_)
                        wrelease(s5, pend[NS - 1])
                        break
                    ncnk = 4 if k < 5 else 2
                    s_, wt_, ld_ = wacquire(rows_src(wd_h, 4 * k, ncnk, half), v_c512)
                    PE.wait(ld_)
                    for fl in range(ncnk):
                        fc = 4 * k + fl
                        PE.wait(hid_ready[fc])
                        for ts in range(NS):
                            last_unit_mm = (fl == ncnk - 1 and ts == NS - 1)
                            pp_ = PE.op(lambda e, fc=fc, fl=fl, ts=ts, wt_=wt_, b_=bks[ts]: e.matmul(
                                b_.ap, lhsT=hid[:, fc, ts * 128:(ts + 1) * 128], rhs=wt_[:, fl, :],
                                start=(fc == 0), stop=(fc == NFC - 1)),
                                inc=(fc == NFC - 1 or last_unit_mm))
                            if fc == NFC - 1:
                                pend[ts] = pp_
                            if last_unit_mm:
                                wrelease(s_, pp_)
                    if half == 1 and passb_hook is not None:
                        passb_hook(k)
                for ts in range(NS):
                    DVE.wait(pend[ts])
                    d_ = DVE.op(lambda e, ts=ts, half=half, b_=bks[ts]: e.scalar_tensor_tensor(
                        out=xres[:, ts, half * 512:(half + 1) * 512], in0=b_.ap, scalar=0.5,
                        in1=xres[:, ts, half * 512:(half + 1) * 512], op0=ALU.mult, op1=ALU.add), nowait=True)
                    bks[ts].rel = [d_]
                    xdep[ts] = d_
                    if half == 1 and evac_hook is not None:
                        evac_hook(ts, xdep)
            return xdep

        def mixer_phase(i, hready, xres, pre_attn_hook=None):
            kbase = (i % 2) * T
            vbase = (i % 2) * 4
            sq_free = [None, None]
            rr_free = [None, None]
            info = {}
            qk_last = None

            SEQ = [0, 4, 1, 5, 2, 6, 3, 7]
            POS = {fq: n for n, fq in enumerate(SEQ)}
            qk_dep = {}

            def proj(fq):
                pos = POS[fq]
                unit = fq // 2
                if ("w", unit) not in info:
                    info[("w", unit)] = wacquire(gate_src(win_h, unit), v_c256)
                s_, wt_, ld_ = info[("w", unit)]
                pb = banks[pos % 4]
                pb.begin(PE)
                if pos == 0:
                    for d_ in hready:
                        PE.wait(d_)
                PE.wait(ld_)
                lo = (fq % 2) * 128
                for c in range(NDC):
                    p = PE.op(lambda e, c=c, pb=pb, wt_=wt_, lo=lo: e.matmul(
                        pb.ap, lhsT=wt_[:, c, lo:lo + 128], rhs=hT[:, c, :], start=(c == 0), stop=(c == NDC - 1)),
                        inc=(c == NDC - 1))
                if fq % 2 == 1:
                    wrelease(s_, p)
                ACT.wait(p)
                ACT.wait(sq_free[pos % 2])
                a = ACT.op(lambda e, pos=pos, pb=pb: e.activation(out=sq[:, pos % 2, :], in_=pb.ap, func=AF.Square),
                           nowait=True)
                info[("sq", fq)] = a

            def ones(fq):
                nonlocal qk_last
                pos = POS[fq]
                pb = banks[pos % 4]
                sbk = banks[4 + pos % 2]
                sbk.begin(PE)
                PE.wait(info[("sq", fq)])
                p2 = PE.op(lambda e, pos=pos, sbk=sbk: e.matmul(sbk.ap, lhsT=bdiag, rhs=sq[:, pos % 2, :],
                                                                start=True, stop=True))
                sq_free[pos % 2] = p2
                ACT.wait(p2)
                ACT.wait(rr_free[pos % 2])
                a2 = ACT.op(lambda e, pos=pos, sbk=sbk: e.activation(out=rr[:, pos % 2, :], in_=sbk.ap, func=AF.Ln,
                                                                     scale=1.0 / 64, bias=epsc[:, 0:1]), nowait=True)
                sbk.rel = [a2]
                a3 = ACT.op(lambda e, pos=pos: e.activation(out=rr[:, pos % 2, :], in_=rr[:, pos % 2, :], func=AF.Exp,
                                                            scale=-0.5))
                DVE.wait(a3)
                if fq < 4:
                    dest = qn[:, fq, :]
                    gc = QG
                else:
                    dest = kT[:, fq - 4, kbase:kbase + T]
                    gc = KG
                d2_ = DVE.op(lambda e, pos=pos, pb=pb, dest=dest, gc=gc: e.scalar_tensor_tensor(
                    out=dest, in0=pb.ap, scalar=prm[:, gc:gc + 1], in1=rr[:, pos % 2, :], op0=ALU.mult, op1=ALU.mult),
                    nowait=True)
                pb.rel = [d2_]
                rr_free[pos % 2] = d2_
                qk_last = d2_
                qk_dep[fq] = d2_

            def run_M1():
                proj(SEQ[0])
                for n in range(1, 8):
                    proj(SEQ[n])
                    ones(SEQ[n - 1])
                ones(SEQ[7])

            mres = {}

            def run_M2():
                v_ready = None
                pend = [None] * NS
                for u in range(2):
                    s_, wt_, ld_ = wacquire(gate_src(win_h, 4 + u), v_c256)
                    PE.wait(ld_)
                    for ts in range(NS):
                        bk = banks[ts]
                        if u == 0:
                            bk.begin(PE)
                        for c in range(NDC):
                            lastmm = (c == NDC - 1)
                            p = PE.op(lambda e, c=c, ts=ts, u=u, bk=bk, wt_=wt_: e.matmul(
                                bk.ap[:, u * 256:(u + 1) * 256], lhsT=hT[:, c, ts * 128:(ts + 1) * 128], rhs=wt_[:, c, :],
                                start=(c == 0), stop=(c == NDC - 1)),
                                inc=(lastmm and (u == 1 or ts == NS - 1)))
                            if lastmm and u == 1:
                                pend[ts] = p
                        if ts == NS - 1:
                            wrelease(s_, p)
                for ts in range(NS):
                    ACT.wait(pend[ts])
                    a = ACT.op(lambda e, ts=ts: e.activation(
                        out=V[:, vbase + ts, :, 0:64], in_=banks[ts].ap.rearrange("p (h d) -> p h d", d=64), func=AF.Copy),
                        nowait=True)
                    banks[ts].rel = [a]
                    mres["v_ready"] = a

            u_ready = [None] * 4
            for fu in range(4):
                if fu % 2 == 0:
                    s_, wt_, ld_ = wacquire(gate_src(win_h, 6 + fu // 2), v_c256)
                bk = banks[4 + fu % 2]
                bk.begin(PE)
                if fu == 0:
                    for d_ in hready:
                        PE.wait(d_)
                PE.wait(ld_)
                lo = (fu % 2) * 128
                for c in range(NDC):
                    p = PE.op(lambda e, c=c, bk=bk, wt_=wt_, lo=lo: e.matmul(
                        bk.ap, lhsT=wt_[:, c, lo:lo + 128], rhs=hT[:, c, :], start=(c == 0), stop=(c == NDC - 1)),
                        inc=(c == NDC - 1))
                if fu % 2 == 1:
                    wrelease(s_, p)
                ACT.wait(p)
                a = ACT.op(lambda e, fu=fu, bk=bk: e.activation(out=uT[:, fu, 16:16 + T], in_=bk.ap, func=AF.Copy),
                           nowait=True)
                bk.rel = [a]
                u_ready[fu] = a

            W_ = 16 + T
            pool_ready = None
            dT_ready = [None] * 4
            for g in range(4):
                DVE.wait(u_ready[g])
                U = uT[:, g, :]
                A = ptmp[:, 0, :]
                B = ptmp[:, 1, :]
                dprev = DVE.op(lambda e, U=U, A=A: e.tensor_tensor(out=A[:, 1:W_], in0=U[:, 1:W_], in1=U[:, 0:W_ - 1],
                                                                   op=ALU.add))
                Sx = A
                if g >= 1:
                    DVE.wait(dprev)
                    dprev = DVE.op(lambda e, A=A, B=B: e.tensor_tensor(out=B[:, 3:W_], in0=A[:, 3:W_], in1=A[:, 1:W_ - 2],
                                                                       op=ALU.add))
                    Sx = B
                if g >= 2:
                    DVE.wait(dprev)
                    dprev = DVE.op(lambda e, A=A, B=B: e.tensor_tensor(out=A[:, 7:W_], in0=B[:, 7:W_], in1=B[:, 3:W_ - 4],
                                                                       op=ALU.add))
                    Sx = A
                if g >= 3:
                    DVE.wait(dprev)
                    dprev = DVE.op(lambda e, A=A, B=B: e.tensor_tensor(out=B[:, 15:W_], in0=A[:, 15:W_], in1=A[:, 7:W_ - 8],
                                                                       op=ALU.add))
                    Sx = B
                DVE.wait(dprev)
                dprev = DVE.op(lambda e, g=g, Sx=Sx, U=U: e.scalar_tensor_tensor(
                    out=dT[:, g, :], in0=Sx[:, 16:W_], scalar=1.0 / POOL_W[g], in1=U[:, 16:W_],
                    op0=ALU.mult, op1=ALU.subtract))
                if i == 0:
                    DVE.wait(dprev)
                    dprev = DVE.op(lambda e, g=g, Sx=Sx: e.tensor_tensor(out=ptmp2[:], in0=Sx[:, 16:32], in1=invc[:, g, :],
                                                                         op=ALU.mult))
                    DVE.wait(dprev)
                    dprev = DVE.op(lambda e, g=g, U=U: e.tensor_tensor(out=dT[:, g, 0:16], in0=ptmp2[:], in1=U[:, 16:32],
                                                                       op=ALU.subtract))
                dT_ready[g] = dprev
            DVE.wait(dprev)
            DVE.op(lambda e: e.tensor_copy(out=uT[:, :, 0:16], in_=uT[:, :, T:T + 16]))

            run_M2()
            v_ready = mres["v_ready"]
            for g in range(4):
                bk = banks[(6, 7, 4, 5)[g]]
                bk.begin(PE)
                PE.wait(dT_ready[g])
                p = PE.op(lambda e, g=g, bk=bk: e.matmul(bk.ap, lhsT=poolw[:, g, :], rhs=dT[:, g, :], start=True, stop=True))
                DVE.wait(p)
                a = DVE.op(lambda e, g=g, bk=bk: e.tensor_scalar(
                    out=apT[:, 4 + g, :], in0=bk.ap, scalar1=prm[:, PSC + g:PSC + g + 1], scalar2=None, op0=ALU.mult),
                    nowait=True)
                bk.rel = [a]
                pool_ready = a
            run_M1()
            qk_ready = qk_last

            hps = [(pp, h) for pp in range(4) for h in range(8)]
            expS_free = [None, None, None]
            PT_free = [None, None, None]
            SPAIR = [(0, 1), (2, 3), (6, 7)]
            atok_free = [None, None]
            pt_ready = {}
            att = {"last": None, "ready": [None] * 4}

            def groups(pp):
                G = 4 * i + pp
                return [(gi, (G - 4 + gi) % 8) for gi in range(5) if G - 4 + gi >= 0]

            splan = {}

            def emit_S_pe(idxs):
                plans = []
                for idx in idxs:
                    pp, h = hps[idx]
                    kc, pr = h // 2, (h % 2) * 64
                    gl = groups(pp)
                    gA = [g_ for g_ in gl if g_[0] < 4]
                    SA = banks[SPAIR[idx % 3][0]]
                    SB = banks[SPAIR[idx % 3][1]]
                    PE.wait(qk_dep[kc])
                    PE.wait(qk_dep[4 + kc])
                    if idx == 0:
                        PE.wait(v_ready)
                    if gA:
                        SA.begin(PE)
                    SB.begin(PE)
                    plans.append(dict(idx=idx, pp=pp, h=h, kc=kc, pr=pr, gl=gl, gA=gA, SA=SA, SB=SB, pA=None, pB=None))
                nmax = max(len(p_["gl"]) for p_ in plans)
                for n_ in range(nmax):
                    for p_ in plans:
                        if n_ >= len(p_["gl"]):
                            continue
                        gi, sl = p_["gl"][n_]
                        isB = (gi == 4)
                        lastA = (not isB) and (gi == p_["gA"][-1][0])
                        out = p_["SB"].ap[:, 0:128] if isB else p_["SA"].ap[:, gi * 128:(gi + 1) * 128]
                        d_ = PE.op(lambda e, out=out, sl=sl, kc=p_["kc"], pr=p_["pr"], pp=p_["pp"]: e.matmul(
                            out, lhsT=kT[pr:pr + 64, kc, sl * 128:(sl + 1) * 128],
                            rhs=qn[pr:pr + 64, kc, pp * 128:(pp + 1) * 128], start=True, stop=True),
                            inc=(isB or lastA))
                        if isB:
                            p_["pB"] = d_
                        elif lastA:
                            p_["pA"] = d_
                for p_ in plans:
                    idx, SA, SB = p_["idx"], p_["SA"], p_["SB"]
                    g0 = p_["gl"][0][0]
                    ACT.wait(expS_free[idx % 2])
                    if p_["gA"]:
                        ACT.wait(p_["pA"])
                        aA = ACT.op(lambda e, idx=idx, SA=SA, g0=g0: e.activation(
                            out=expS[:, idx % 2, g0 * 128:512], in_=SA.ap[:, g0 * 128:512], func=AF.Exp, scale=0.125),
                            nowait=True)
                        SA.rel = [aA]
                    ACT.wait(p_["pB"])
                    aB = ACT.op(lambda e, idx=idx, SB=SB: e.activation(
                        out=expS[:, idx % 2, 512:640], in_=SB.ap[:, 0:128], func=AF.Exp, scale=0.125), nowait=True)
                    SB.rel = [aB]
                    splan[idx] = (aB, g0, p_["h"])

            def emit_S_dve(idx):
                aB, g0, h = splan.pop(idx)
                DVE.wait(PT_free[idx % 3])
                DVE.wait(aB)
                d_ = DVE.op(lambda e, idx=idx, h=h, g0=g0: e.tensor_tensor(
                    out=PT[:, idx % 3, g0 * 128:640], in0=expS[:, idx % 2, g0 * 128:640],
                    in1=etab[:, h, g0:5, :].rearrange("p g q -> p (g q)"), op=ALU.mult), nowait=True)
                expS_free[idx % 2] = d_
                pt_ready[idx] = d_

            def emit_PV(idx):
                pp, h = hps[idx]
                gl = groups(pp)
                OB = banks[4 + h // 4]
                if h % 4 == 0:
                    OB.begin(PE)
                PE.wait(pt_ready[idx])
                c0 = (h % 4) * 65
                for n_, (gi, sl) in enumerate(gl):
                    pO = PE.op(lambda e, gi=gi, sl=sl, OB=OB, c0=c0, idx=idx, h=h, n_=n_: e.matmul(
                        OB.ap[:, c0:c0 + 65], lhsT=PT[:, idx % 3, gi * 128:(gi + 1) * 128], rhs=V[:, sl, h, :],
                        start=(n_ == 0), stop=(n_ == len(gl) - 1)), inc=(n_ == len(gl) - 1))
                PT_free[idx % 3] = pO
                if h % 4 == 3:
                    hg = h // 4
                    rk = (pp % 2) * 2 + hg
                    obv = OB.ap[:, 0:260].rearrange("p (h e) -> p h e", e=65)
                    DVE.wait(pO)
                    if hg == 0:
                        DVE.wait(atok_free[pp % 2])
                    d1 = DVE.op(lambda e, rk=rk, obv=obv: e.reciprocal(out=rec[:, rk * 4:rk * 4 + 4].unsqueeze(2),
                                                                       in_=obv[:, :, 64:65]), nowait=True)
                    DVE.wait(d1)
                    d2_ = DVE.op(lambda e, rk=rk, obv=obv, pp=pp, hg=hg: e.tensor_tensor(
                        out=atok[:, pp % 2, hg * 256:(hg + 1) * 256].rearrange("p (h d) -> p h d", d=64),
                        in0=obv[:, :, 0:64],
                        in1=rec[:, rk * 4:rk * 4 + 4].unsqueeze(2).broadcast_to([128, 4, 64]), op=ALU.mult))
                    OB.rel = [d2_]
                    if hg == 1:
                        att["pending"] = (idx, pp, d2_)

            def emit_TP(pp, d2_):
                tb = banks[5]
                tb.begin(PE)
                PE.wait(d2_)
                tpb = tb.ap.bitcast(BF16)
                for fc in range(4):
                    pT_ = PE.op(lambda e, fc=fc, pp=pp, tpb=tpb: e.transpose(
                        out=tpb[:, fc * 128:(fc + 1) * 128], in_=atok[:, pp % 2, fc * 128:(fc + 1) * 128],
                        identity=ident), inc=(fc == 3))
                atok_free[pp % 2] = pT_
                ACT.wait(pT_)
                a = ACT.op(lambda e, pp=pp, tpb=tpb: e.activation(
                    out=apT[:, 0:4, pp * 128:(pp + 1) * 128],
                    in_=tpb[:, 0:512].rearrange("p (c t) -> p c t", c=4), func=AF.Copy), nowait=True)
                tb.rel = [a]
                att["last"] = a
                att["ready"][pp] = a

            if pre_attn_hook is not None:
                pre_attn_hook(qk_ready)
            def after_pv(idx):
                pend_tp = att.get("pending")
                if pend_tp is not None and (idx >= pend_tp[0] + 2 or idx == len(hps) - 1):
                    emit_TP(pend_tp[1], pend_tp[2])
                    att["pending"] = None

            emit_S_pe([0, 1])
            emit_S_dve(0)
            emit_S_dve(1)
            for idx in range(0, len(hps), 2):
                nxt_ = [j_ for j_ in (idx + 2, idx + 3) if j_ < len(hps)]
                if nxt_:
                    emit_S_pe(nxt_)
                emit_PV(idx)
                after_pv(idx)
                for j_ in nxt_:
                    emit_S_dve(j_)
                emit_PV(idx + 1)
                after_pv(idx + 1)
            attn_ready = att["last"]
            ACT.op(lambda e: e.activation(out=st[:, 44:45], in_=st[:, 45:46], func=AF.Sqrt))

            xdep = [None] * NS
            for half in range(2):
                bks = [banks[t_] for t_ in range(NS)] if half == 0 else [banks[4 + t_] for t_ in range(NS)]
                for b_ in bks:
                    b_.begin(PE)
                pend = [None] * NS
                s0, w0, l0 = wacquire(rows_src(wout_h, 0, 4, half), v_c512)
                s1, w1, l1 = wacquire(rows_src(wout_h, 4, 4, half), v_c512)
                PE.wait(l0)
                PE.wait(l1)
                PE.wait(pool_ready)
                for ts in range(NS):
                    PE.wait(att["ready"][ts])
                    for fc in range(8):
                        wt_, fl = (w0, fc) if fc < 4 else (w1, fc - 4)
                        rel0 = (ts == NS - 1 and fc == 3)
                        pp_ = PE.op(lambda e, fc=fc, fl=fl, ts=ts, wt_=wt_, b_=bks[ts]: e.matmul(
                            b_.ap, lhsT=apT[:, fc, ts * 128:(ts + 1) * 128], rhs=wt_[:, fl, :],
                            start=(fc == 0), stop=(fc == 7)), inc=(fc == 7 or rel0))
                        if fc == 7:
                            pend[ts] = pp_
                        if rel0:
                            wrelease(s0, pp_)
                wrelease(s1, pend[NS - 1])
                for ts in range(NS):
                    DVE.wait(pend[ts])
                    d_ = DVE.op(lambda e, ts=ts, half=half, b_=bks[ts]: e.tensor_tensor(
                        out=xres[:, ts, half * 512:(half + 1) * 512], in0=b_.ap,
                        in1=xres[:, ts, half * 512:(half + 1) * 512], op=ALU.add), nowait=True)
                    bks[ts].rel = [d_]
                    xdep[ts] = d_
            return xdep

        ost_cnt = [0, 0, 0, 0]
        pending_stores = []
        n4_done = {}

        def xload(i):
            xb = i % 2
            if i - 2 in n4_done:
                SP.wait(n4_done[i - 2][0])
                SP.wait(n4_done[i - 2][1])
            return SP.dma(lambda e, i=i, xb=xb: e.dma_start(
                out=xres2[:, xb], in_=x_h[i * T:(i + 1) * T, :].rearrange("(s p) d -> p s d", p=128)), xld[xb])

        n4_act_dep = {}
        n4_sq_dep = {}

        def n4_act(i, ts, xdep, with_sqrt=True):
            xr = xres2[:, i % 2]
            k = 12 + ts
            ACT.wait(xdep[ts])
            a1 = ACT.op(lambda e, ts=ts, k=k, xr=xr: e.activation(out=junk[:], in_=xr[:, ts, :], func=AF.Square,
                                                                  accum_out=st[:, k:k + 1]))
            n4_sq_dep[(i, ts)] = a1
            if with_sqrt:
                a2 = ACT.op(lambda e, k=k: e.activation(out=st[:, 16 + k:17 + k], in_=st[:, k:k + 1], func=AF.Sqrt,
                                                        scale=1.0 / D, bias=epsc[:, 0:1]))
                n4_act_dep[(i, ts)] = a2

        def n4_sqrt_all(i):
            a2 = ACT.op(lambda e: e.activation(out=st[:, 28:32], in_=st[:, 12:16], func=AF.Sqrt,
                                               scale=1.0 / D, bias=epsc[:, 0:1]))
            for ts in range(NS):
                n4_act_dep[(i, ts)] = a2

        def n4_dve(i, ts, xdep):
            xr = xres2[:, i % 2]
            k = 12 + ts
            a2 = n4_act_dep[(i, ts)]
            DVE.wait(c_all)
            DVE.wait(a2)
            d1 = DVE.op(lambda e, k=k: e.reciprocal(out=st[:, 32 + k:33 + k], in_=st[:, 16 + k:17 + k]))
            sl = ts
            DVE.wait((ost[sl], ost_cnt[sl]))
            DVE.wait(xdep[ts])
            d2_ = DVE.op(lambda e, ts=ts, k=k, sl=sl, xr=xr: e.scalar_tensor_tensor(
                out=ostage[:, sl, :], in0=xr[:, ts, :], scalar=st[:, 32 + k:33 + k], in1=gfin[:],
                op0=ALU.mult, op1=ALU.mult))
            ost_cnt[sl] += 16
            pending_stores.append((i, ts, sl, d2_))
            n4_done[i] = (a2, d2_)
            if i + 1 == NT:
                flush_stores()

        def flush_stores():
            while pending_stores:
                i, ts, sl, d2_ = pending_stores.pop(0)
                SP.wait(d2_)
                SP.dma(lambda e, i=i, ts=ts, sl=sl: e.dma_start(
                    out=y_h[i * T + ts * 128:i * T + (ts + 1) * 128, :], in_=ostage[:, sl, :]), ost[sl])

        def n4_piece(i, ts, xdep):
            n4_act(i, ts, xdep)
            n4_dve(i, ts, xdep)

        xl = {0: xl0}
        hready = None
        final_xdep = {}
        for i in range(NT):
            xr = xres2[:, i % 2]
            if i == 0:
                hready = norm_phase(0, G1, [xl[0]] * NS, xr)
            if i > 0:
                def hook1(j, i=i):
                    if j % 2 == 1 and j < 2 * NS:
                        n4_act(i - 1, (j - 1) // 2, final_xdep[i - 1], with_sqrt=False)
                        if j == 2 * NS - 1:
                            n4_sqrt_all(i - 1)
                    if 2 * NS <= j < 3 * NS:
                        n4_dve(i - 1, j - 2 * NS, final_xdep[i - 1])
            else:
                def hook1(j):
                    if 2 <= j < 10:
                        etab_exp(j - 2)
            xdep = ffn_phase(wg1_h, wu1_h, wd1_h, hready, xr, chunk_hook=hook1)
            hready = norm_phase(1, GM, xdep, xr)
            if i == 0:
                DVE.wait(c_all)
                DVE.wait(etab_state["ready"])
            pah = None
            if i + 1 < NT:
                def pah(dep, i=i):
                    SP.wait(dep)
                    xl[i + 1] = xload(i + 1)
                    flush_stores()
            xdep = mixer_phase(i, hready, xr, pre_attn_hook=pah)
            hready = norm_phase(2, G2, xdep, xr)
            agu = pbh = None
            nxt = {}
            if i + 1 < NT:
                sa, sb_, rdy = make_norm(0, G1, [xl[i + 1]] * NS, xres2[:, (i + 1) % 2])
                nxt["ready"] = rdy

                def agu(sa=sa):
                    sa(0)
                    sa(1)

                def pbh(k, sa=sa, sb_=sb_):
                    if k < NS:
                        sb_(k)
                        if k + 2 < NS:
                            sa(k + 2)
            evh = None
            if i + 1 == NT:
                def evh(ts, xdep_, i=i):
                    DVE.wait(c_all)
                    n4_piece(i, ts, xdep_)
            xdep = ffn_phase(wg2_h, wu2_h, wd2_h, hready, xr, after_gu_hook=agu, passb_hook=pbh, evac_hook=evh)
            final_xdep[i] = xdep
            if i + 1 < NT:
                hready = nxt["ready"]
        flush_stores()
        for k in range(4):
            SP.wait((ost[k], ost_cnt[k]))

        @block.sync
        def _(e):
            SP.replay(e)

        @block.gpsimd
        def _(e):
            POOL.replay(e)

        @block.scalar
        def _(e):
            ACT.replay(e)

        @block.vector
        def _(e):
            DVE.replay(e)

        @block.tensor
        def _(e):
            PE.replay(e)
    return nc


def host_consts(q_norm, k_norm, rel_bias, pool_w, pool_scale, ffn1_norm, mix_norm, ffn2_norm, final_norm):
    f32 = np.float32
    prm = np.zeros((128, 32), f32)
    prm[:, G1:G1 + 8] = np.asarray(ffn1_norm, f32).reshape(8, 128).T
    prm[:, GM:GM + 8] = np.asarray(mix_norm, f32).reshape(8, 128).T
    prm[:, G2:G2 + 8] = np.asarray(ffn2_norm, f32).reshape(8, 128).T
    prm[:, QG] = np.tile(np.asarray(q_norm, f32), 2)
    prm[:, KG] = np.tile(np.asarray(k_norm, f32), 2)
    prm[:, PSC:PSC + 4] = np.asarray(pool_scale, f32).reshape(4, 128).T
    gfin = np.ascontiguousarray(np.broadcast_to(np.asarray(final_norm, f32)[None, :], (128, D)))
    cb = np.zeros((128, 256), f32)
    cb[:, 0:128] = np.eye(128, dtype=f32)
    blk = np.arange(128) // 64
    cb[:, 128:256] = (blk[:, None] == blk[None, :]).astype(f32)
    invc = np.zeros((128, 4, 16), f32)
    for g, w in enumerate(POOL_W):
        invc[:, g, :] = (1.0 / np.minimum(np.arange(16) + 1, w)).astype(f32)[None, :]
    invc = invc.reshape(128, 64)
    kj = np.arange(640)[:, None]
    qi = np.arange(128)[None, :]
    idx = np.clip(512 + qi - kj, -128, 128) + 128
    dchunk = kj // 64 - qi // 64
    valid = (dchunk >= 0) & (dchunk <= 8)
    rb = np.asarray(rel_bias, f32)
    bt = rb[:, idx]
    bt = np.where(valid[None], bt, f32(-200.0))
    bt = bt.reshape(8, 5, 128, 128).transpose(2, 0, 1, 3)
    biasT = np.ascontiguousarray(bt).reshape(128, 8 * 5 * 128).astype(f32)
    pw = np.asarray(pool_w, f32).transpose(1, 0, 2).reshape(128, 512)
    return dict(prm=prm, gfin=gfin, cb=cb, invc=invc, biasT=biasT, poolw=np.ascontiguousarray(pw))


_NC_CACHE = {}


def make_in_maps(inputs, S):
    f32 = np.float32
    c = host_consts(inputs["q_norm"][0], inputs["k_norm"][0], inputs["rel_bias"][0], inputs["pool_w"][0],
                    inputs["pool_scale"][0], inputs["ffn1_norm"][0], inputs["mix_norm"][0], inputs["ffn2_norm"][0],
                    inputs["final_norm"][0])
    shared = dict(
        wg1=np.ascontiguousarray(inputs["ffn1_w_gate"][0], f32), wu1=np.ascontiguousarray(inputs["ffn1_w_up"][0], f32),
        wd1=np.ascontiguousarray(inputs["ffn1_w_down"][0], f32), wg2=np.ascontiguousarray(inputs["ffn2_w_gate"][0], f32),
        wu2=np.ascontiguousarray(inputs["ffn2_w_up"][0], f32), wd2=np.ascontiguousarray(inputs["ffn2_w_down"][0], f32),
        win=np.ascontiguousarray(inputs["w_in"][0], f32), wout=np.ascontiguousarray(inputs["w_out"][0], f32), **c)
    x = np.asarray(inputs["x"], f32)
    return [dict(x=np.ascontiguousarray(x[b, :S]), **shared) for b in range(x.shape[0])]


def kernel(**inputs):
    inputs = {k: np.asarray(v) for k, v in inputs.items()}
    B, S, _ = inputs["x"].shape
    if S not in _NC_CACHE:
        _NC_CACHE[S] = build_program(S)
    nc = _NC_CACHE[S]
    in_maps = make_in_maps(inputs, S)
    res = run_bass_kernel_spmd(nc, in_maps, core_ids=list(range(B)))
    return np.stack([np.asarray(r["y"], np.float32) for r in res.results], axis=0)
```
